# Optimizing a Trainium2 kernel written in Bass

```python
import math
import jax, jax.numpy as jnp
from jax import lax
import numpy as np

D_MODEL = 1024
BATCH = 4
SEQ = 4096
DEPTH = 2
DEC_BATCH = 32
DEC_SEQ = 8
PAST_LEN = 8192
PAGE_SIZE = 128

N_AB_LAYERS = (DEPTH + 1) // 2
N_LRU_LAYERS = DEPTH // 2
H_FOX = 8
DH_FOX = 64
H_DIFF = 4
DH_DIFF = 64
D_FOX = H_FOX * DH_FOX
D_DIFF = H_DIFF * 2 * DH_DIFF
D_IN_AB = 3 * D_FOX + H_FOX + 3 * D_DIFF
D_MIX_AB = D_FOX + D_DIFF
ROT_DIMS = DH_DIFF // 4
ROPE_THETA = 500000.0
Q_BLOCK = 128
D_RNN = 1280
N_LRU_BLOCKS = 16
LRU_BW = D_RNN // N_LRU_BLOCKS
LRU_CONV_W = 4
LRU_C = 8.0
D_FF = 3072
FFN_CONV_W = 3
EPS = 1e-6
NEG = -1e30

kernel_name = "fox_diffattn_rglru_convffn_step"


def _rmsnorm(x, g):
    xf = x.astype(jnp.float32)
    y = xf * lax.rsqrt(jnp.mean(xf * xf, axis=-1, keepdims=True) + EPS)
    return (y * g.astype(jnp.float32)).astype(x.dtype)


def _rope_partial(x, pos):
    half = ROT_DIMS // 2
    inv = ROPE_THETA ** (-jnp.arange(half, dtype=jnp.float32) / half)
    ang = pos.astype(jnp.float32)[:, None] * inv[None, :]
    cos = jnp.cos(ang)[None, :, None, None, :]
    sin = jnp.sin(ang)[None, :, None, None, :]
    xr = x[..., :ROT_DIMS].astype(jnp.float32)
    x1, x2 = xr[..., :half], xr[..., half:]
    rot = jnp.concatenate([x1 * cos - x2 * sin, x2 * cos + x1 * sin], axis=-1).astype(x.dtype)
    return jnp.concatenate([rot, x[..., ROT_DIMS:]], axis=-1)


def _causal_dwconv(u, buf, w, b):
    W = w.shape[0]
    T = u.shape[1]
    up = jnp.concatenate([buf.astype(u.dtype), u], axis=1)
    y = b
    for k in range(W):
        y = y + up[:, k:k + T] * w[k]
    return y, up[:, -(W - 1):]


def _ab_attend(qf, cq, qd, qpos, segs, lam, lam_init, subln_g):
    sf, sd, lens = [], [], []
    for kf, vf, ck, kd, vd, kpos in segs:
        mask = kpos[None, :] <= qpos[:, None]
        s = jnp.einsum('bqhd,bkhd->bhqk', qf, kf).astype(jnp.float32) * (DH_FOX ** -0.5)
        s = s + jnp.transpose(cq, (0, 2, 1))[:, :, :, None] - jnp.transpose(ck, (0, 2, 1))[:, :, None, :]
        sf.append(jnp.where(mask, s, NEG))
        s2 = jnp.einsum('bqhnd,bkhnd->bhnqk', qd, kd).astype(jnp.float32) * (DH_DIFF ** -0.5)
        sd.append(jnp.where(mask, s2, NEG))
        lens.append(kf.shape[1])
    pf = jax.nn.softmax(jnp.concatenate(sf, axis=-1), axis=-1)
    pd = jax.nn.softmax(jnp.concatenate(sd, axis=-1), axis=-1)
    pdiff = pd[:, :, 0] - lam * pd[:, :, 1]
    splits = [sum(lens[:i + 1]) for i in range(len(lens) - 1)]
    of = 0.0
    od = 0.0
    for pfs, pds, seg in zip(jnp.split(pf, splits, axis=-1), jnp.split(pdiff, splits, axis=-1), segs):
        vf, vd = seg[1], seg[4]
        of = of + jnp.einsum('bhqk,bkhd->bqhd', pfs.astype(vf.dtype), vf)
        od = od + jnp.einsum('bhqk,bkhe->bqhe', pds.astype(vd.dtype), vd)
    od = _rmsnorm(od, subln_g) * (1.0 - lam_init)
    b, q = qf.shape[:2]
    return jnp.concatenate([of.reshape(b, q, D_FOX), od.reshape(b, q, D_DIFF)], axis=-1)


def _ab_mixer(h, pos, past, j, layer_idx, W):
    b, t, _ = h.shape
    proj = h @ W['ab_w_in'][j]
    idx = [D_FOX, 2 * D_FOX, 3 * D_FOX, 3 * D_FOX + H_FOX,
           3 * D_FOX + H_FOX + D_DIFF, 3 * D_FOX + H_FOX + 2 * D_DIFF]
    qf, kf, vf, fl, qd, kd, vd = jnp.split(proj, idx, axis=-1)
    qf = qf.reshape(b, t, H_FOX, DH_FOX)
    kf = kf.reshape(b, t, H_FOX, DH_FOX)
    vf = vf.reshape(b, t, H_FOX, DH_FOX)
    logf = jax.nn.log_sigmoid((fl + W['ab_b_f'][j]).astype(jnp.float32))
    qd = _rope_partial(qd.reshape(b, t, H_DIFF, 2, DH_DIFF), pos)
    kd = _rope_partial(kd.reshape(b, t, H_DIFF, 2, DH_DIFF), pos)
    vd = vd.reshape(b, t, H_DIFF, 2 * DH_DIFF)
    lam_init = 0.8 - 0.6 * math.exp(-0.3 * layer_idx)
    lq1, lk1 = W['ab_lam_q1'][j].astype(jnp.float32), W['ab_lam_k1'][j].astype(jnp.float32)
    lq2, lk2 = W['ab_lam_q2'][j].astype(jnp.float32), W['ab_lam_k2'][j].astype(jnp.float32)
    lam = jnp.exp(jnp.sum(lq1 * lk1)) - jnp.exp(jnp.sum(lq2 * lk2)) + lam_init
    g = W['ab_subln_g'][j]
    if past is None:
        c = jnp.cumsum(logf, axis=1)
        segs = [(kf, vf, c, kd, vd, pos)]
        if t % Q_BLOCK == 0 and t > Q_BLOCK:
            nb = t // Q_BLOCK
            def to_blocks(a):
                return jnp.moveaxis(a.reshape((b, nb, Q_BLOCK) + a.shape[2:]), 1, 0)
            def blk(xs):
                qf_b, cq_b, qd_b, qpos_b = xs
                return _ab_attend(qf_b, cq_b, qd_b, qpos_b, segs, lam, lam_init, g)
            out = lax.map(blk, (to_blocks(qf), to_blocks(c), to_blocks(qd), pos.reshape(nb, Q_BLOCK)))
            out = jnp.moveaxis(out, 0, 1).reshape(b, t, D_MIX_AB)
        else:
            out = _ab_attend(qf, c, qd, pos, segs, lam, lam_init, g)
    else:
        kf_p, vf_p, logf_p, kd_p, vd_p = past
        P = kf_p.shape[1]
        c_all = jnp.cumsum(jnp.concatenate([logf_p.astype(jnp.float32), logf], axis=1), axis=1)
        c_p, c_new = c_all[:, :P], c_all[:, P:]
        segs = [(kf_p, vf_p, c_p, kd_p, vd_p, jnp.arange(P)),
                (kf, vf, c_new, kd, vd, pos)]
        out = _ab_attend(qf, c_new, qd, pos, segs, lam, lam_init, g)
    y = out @ W['ab_w_out'][j]
    rows = (kf, vf, logf.astype(h.dtype), kd.reshape(b, t, H_DIFF, 2 * DH_DIFF), vd)
    return y, rows


def _lru_mixer(h, conv_buf, h_prev, j, W):
    b, t, _ = h.shape
    gate = jax.nn.gelu(h @ W['lru_w_gate'][j])
    u = h @ W['lru_w_x'][j]
    xc, new_buf = _causal_dwconv(u, conv_buf, W['lru_conv_w'][j], W['lru_conv_b'][j])
    xb = xc.reshape(b, t, N_LRU_BLOCKS, LRU_BW)
    r = jax.nn.sigmoid(jnp.einsum('btnc,ncd->btnd', xb, W['lru_w_a'][j]).reshape(b, t, D_RNN) + W['lru_b_a'][j])
    i = jax.nn.sigmoid(jnp.einsum('btnc,ncd->btnd', xb, W['lru_w_i'][j]).reshape(b, t, D_RNN) + W['lru_b_i'][j])
    log_a = LRU_C * r.astype(jnp.float32) * jax.nn.log_sigmoid(W['lru_lambda'][j].astype(jnp.float32))
    a = jnp.exp(log_a)
    mult = jnp.sqrt(-jnp.expm1(2.0 * log_a))
    bx = mult * (i * xc).astype(jnp.float32)
    bx = bx.at[:, 0].add(a[:, 0] * h_prev.astype(jnp.float32))

    def comb(l, rr):
        a1, b1 = l
        a2, b2 = rr
        return a1 * a2, a2 * b1 + b2

    _, hs = lax.associative_scan(comb, (a, bx), axis=1)
    y = (hs.astype(h.dtype) * gate) @ W['lru_w_out'][j]
    return y, new_buf, hs[:, -1].astype(h.dtype)


def _conv_ffn(h, buf, l, W):
    a = h @ W['ffn_w_a'][l]
    g = h @ W['ffn_w_b'][l]
    ac, new_buf = _causal_dwconv(a, buf, W['ffn_conv_w'][l], W['ffn_conv_b'][l])
    return (jax.nn.gelu(ac) * g) @ W['ffn_w_down'][l], new_buf


def _trunk(x, pos, gather_past, lru_conv0, lru_h0, ffn_buf0, W):
    fk, fv, fl, dk, dv, lc, lh, fc = [], [], [], [], [], [], [], []
    for l in range(DEPTH):
        j = l // 2
        h = _rmsnorm(x, W['mix_norm_g'][l])
        if l % 2 == 0:
            past = None if gather_past is None else gather_past(j)
            y, rows = _ab_mixer(h, pos, past, j, l, W)
            fk.append(rows[0]); fv.append(rows[1]); fl.append(rows[2]); dk.append(rows[3]); dv.append(rows[4])
        else:
            y, nbuf, nh = _lru_mixer(h, lru_conv0[j], lru_h0[j], j, W)
            lc.append(nbuf); lh.append(nh)
        x = x + y
        h = _rmsnorm(x, W['ffn_norm_g'][l])
        y, nbuf = _conv_ffn(h, ffn_buf0[l], l, W)
        fc.append(nbuf)
        x = x + y
    x = _rmsnorm(x, W['final_norm_g'])
    return x, (jnp.stack(fk), jnp.stack(fv), jnp.stack(fl), jnp.stack(dk), jnp.stack(dv),
               jnp.stack(lc), jnp.stack(lh), jnp.stack(fc))


def setup_inputs(seed: int = 0) -> dict:
    key = jax.random.key(seed)
    keys = list(jax.random.split(key, 48))
    cnt = [0]

    def nk():
        k = keys[cnt[0]]
        cnt[0] += 1
        return k

    def nrm(shape, scale):
        return jax.random.normal(nk(), shape, jnp.float32) * scale

    n_pages = PAST_LEN // PAGE_SIZE
    n_used = DEC_BATCH * n_pages
    n_pool = n_used + n_used // 4
    page_table = jax.random.permutation(nk(), n_pool)[:n_used].reshape(DEC_BATCH, n_pages).astype(jnp.int32)

    a0 = jax.random.uniform(nk(), (N_LRU_LAYERS, D_RNN), jnp.float32, 0.9, 0.999)
    s = a0 ** (1.0 / LRU_C)
    lru_lambda = jnp.log(s) - jnp.log1p(-s)

    return {
        "x_prompt": nrm((BATCH, SEQ, D_MODEL), 1.0),
        "x_sample": nrm((DEC_BATCH, DEC_SEQ, D_MODEL), 1.0),
        "cache_fox_k": nrm((N_AB_LAYERS, n_pool, PAGE_SIZE, H_FOX, DH_FOX), 1.0),
        "cache_fox_v": nrm((N_AB_LAYERS, n_pool, PAGE_SIZE, H_FOX, DH_FOX), 1.0),
        "cache_fox_logf": jax.nn.log_sigmoid(3.0 + nrm((N_AB_LAYERS, n_pool, PAGE_SIZE, H_FOX), 0.5)),
        "cache_diff_k": nrm((N_AB_LAYERS, n_pool, PAGE_SIZE, H_DIFF, 2 * DH_DIFF), 1.0),
        "cache_diff_v": nrm((N_AB_LAYERS, n_pool, PAGE_SIZE, H_DIFF, 2 * DH_DIFF), 1.0),
        "state_lru_conv": nrm((N_LRU_LAYERS, DEC_BATCH, LRU_CONV_W - 1, D_RNN), 1.0),
        "state_lru_h": nrm((N_LRU_LAYERS, DEC_BATCH, D_RNN), 0.5),
        "state_ffn_conv": nrm((DEPTH, DEC_BATCH, FFN_CONV_W - 1, D_FF), 1.0),
        "page_table": page_table,
        "mix_norm_g": 1.0 + nrm((DEPTH, D_MODEL), 0.05),
        "ab_w_in": nrm((N_AB_LAYERS, D_MODEL, D_IN_AB), D_MODEL ** -0.5),
        "ab_b_f": 3.0 + nrm((N_AB_LAYERS, H_FOX), 0.5),
        "ab_lam_q1": nrm((N_AB_LAYERS, DH_DIFF), 0.1),
        "ab_lam_k1": nrm((N_AB_LAYERS, DH_DIFF), 0.1),
        "ab_lam_q2": nrm((N_AB_LAYERS, DH_DIFF), 0.1),
        "ab_lam_k2": nrm((N_AB_LAYERS, DH_DIFF), 0.1),
        "ab_subln_g": 1.0 + nrm((N_AB_LAYERS, 2 * DH_DIFF), 0.05),
        "ab_w_out": nrm((N_AB_LAYERS, D_MIX_AB, D_MODEL), D_MIX_AB ** -0.5),
        "lru_w_gate": nrm((N_LRU_LAYERS, D_MODEL, D_RNN), D_MODEL ** -0.5),
        "lru_w_x": nrm((N_LRU_LAYERS, D_MODEL, D_RNN), D_MODEL ** -0.5),
        "lru_conv_w": nrm((N_LRU_LAYERS, LRU_CONV_W, D_RNN), LRU_CONV_W ** -0.5),
        "lru_conv_b": nrm((N_LRU_LAYERS, D_RNN), 0.01),
        "lru_w_a": nrm((N_LRU_LAYERS, N_LRU_BLOCKS, LRU_BW, LRU_BW), LRU_BW ** -0.5),
        "lru_b_a": nrm((N_LRU_LAYERS, D_RNN), 0.01),
        "lru_w_i": nrm((N_LRU_LAYERS, N_LRU_BLOCKS, LRU_BW, LRU_BW), LRU_BW ** -0.5),
        "lru_b_i": nrm((N_LRU_LAYERS, D_RNN), 0.01),
        "lru_lambda": lru_lambda,
        "lru_w_out": nrm((N_LRU_LAYERS, D_RNN, D_MODEL), D_RNN ** -0.5),
        "ffn_norm_g": 1.0 + nrm((DEPTH, D_MODEL), 0.05),
        "ffn_w_a": nrm((DEPTH, D_MODEL, D_FF), D_MODEL ** -0.5),
        "ffn_w_b": nrm((DEPTH, D_MODEL, D_FF), D_MODEL ** -0.5),
        "ffn_conv_w": nrm((DEPTH, FFN_CONV_W, D_FF), FFN_CONV_W ** -0.5),
        "ffn_conv_b": nrm((DEPTH, D_FF), 0.01),
        "ffn_w_down": nrm((DEPTH, D_FF, D_MODEL), D_FF ** -0.5),
        "final_norm_g": 1.0 + nrm((D_MODEL,), 0.05),
    }


def reference(x_prompt, x_sample, cache_fox_k, cache_fox_v, cache_fox_logf, cache_diff_k, cache_diff_v,
              state_lru_conv, state_lru_h, state_ffn_conv, page_table,
              mix_norm_g, ab_w_in, ab_b_f, ab_lam_q1, ab_lam_k1, ab_lam_q2, ab_lam_k2, ab_subln_g, ab_w_out,
              lru_w_gate, lru_w_x, lru_conv_w, lru_conv_b, lru_w_a, lru_b_a, lru_w_i, lru_b_i, lru_lambda, lru_w_out,
              ffn_norm_g, ffn_w_a, ffn_w_b, ffn_conv_w, ffn_conv_b, ffn_w_down, final_norm_g):
    W = dict(mix_norm_g=mix_norm_g, ab_w_in=ab_w_in, ab_b_f=ab_b_f, ab_lam_q1=ab_lam_q1, ab_lam_k1=ab_lam_k1,
             ab_lam_q2=ab_lam_q2, ab_lam_k2=ab_lam_k2, ab_subln_g=ab_subln_g, ab_w_out=ab_w_out,
             lru_w_gate=lru_w_gate, lru_w_x=lru_w_x, lru_conv_w=lru_conv_w, lru_conv_b=lru_conv_b,
             lru_w_a=lru_w_a, lru_b_a=lru_b_a, lru_w_i=lru_w_i, lru_b_i=lru_b_i, lru_lambda=lru_lambda,
             lru_w_out=lru_w_out, ffn_norm_g=ffn_norm_g, ffn_w_a=ffn_w_a, ffn_w_b=ffn_w_b,
             ffn_conv_w=ffn_conv_w, ffn_conv_b=ffn_conv_b, ffn_w_down=ffn_w_down, final_norm_g=final_norm_g)

    bp, tp, _ = x_prompt.shape
    pos_p = jnp.arange(tp)
    z_lc = jnp.zeros((N_LRU_LAYERS, bp, LRU_CONV_W - 1, D_RNN), x_prompt.dtype)
    z_lh = jnp.zeros((N_LRU_LAYERS, bp, D_RNN), x_prompt.dtype)
    z_fc = jnp.zeros((DEPTH, bp, FFN_CONV_W - 1, D_FF), x_prompt.dtype)
    y_prompt, st_p = _trunk(x_prompt, pos_p, None, z_lc, z_lh, z_fc, W)
    p_fox_k, p_fox_v, p_fox_logf, p_diff_k, p_diff_v, p_lru_conv, p_lru_h, p_ffn_conv = st_p

    bs, ts, _ = x_sample.shape
    P = page_table.shape[1] * PAGE_SIZE
    pos_s = P + jnp.arange(ts)

    def gather_past(j):
        def g(c):
            return c[j][page_table].reshape((bs, P) + c.shape[3:])
        return (g(cache_fox_k), g(cache_fox_v), g(cache_fox_logf),
                g(cache_diff_k).reshape(bs, P, H_DIFF, 2, DH_DIFF), g(cache_diff_v))

    y_sample, st_s = _trunk(x_sample, pos_s, gather_past, state_lru_conv, state_lru_h, state_ffn_conv, W)
    s_fox_k, s_fox_v, s_fox_logf, s_diff_k, s_diff_v, s_lru_conv, s_lru_h, s_ffn_conv = st_s

    return (y_prompt, y_sample,
            p_fox_k, p_fox_v, p_fox_logf, p_diff_k, p_diff_v, p_lru_conv, p_lru_h, p_ffn_conv,
            s_fox_k, s_fox_v, s_fox_logf, s_diff_k, s_diff_v, s_lru_conv, s_lru_h, s_ffn_conv)
```

```python
import contextlib
import math
import numpy as np
import ml_dtypes
import concourse.bass as bass
import concourse.mybir as mybir
from concourse.bass_utils import run_bass_kernel_spmd

F32 = mybir.dt.float32
BF16 = mybir.dt.bfloat16
I32 = mybir.dt.int32
ALU = mybir.AluOpType
AF = mybir.ActivationFunctionType
AX = mybir.AxisListType

D = 1024
L = 4096
NTP = 32
NPOOL = 2560
DIN = 3080
DRNN = 1280
DFF = 3072
EPS = 1e-6
LAM_INIT0 = 0.8 - 0.6 * math.exp(-0.3 * 0)
SK = 65 * 128


class Prog:
    def __init__(self, nc):
        self.nc = nc
        self.ops = []

    def op(self, eng, fn, reads=(), writes=(), dma=False, key=None):
        self.ops.append(dict(eng=eng, fn=fn, reads=tuple(reads), writes=tuple(writes), dma=dma, key=key))

    def emit(self):
        ops = self.ops
        cnt, kcnt = {}, {}
        for o in ops:
            if o["dma"]:
                k = o["key"]
                if k is None:
                    k = o["key"] = ("dk", (o["writes"] + o["reads"])[0])
                kcnt[k] = kcnt.get(k, 0) + 1
                o["tok"] = ("k", k, kcnt[k])
            else:
                e = o["eng"]
                cnt[e] = cnt.get(e, 0) + 1
                o["tok"] = ("e", e, cnt[e])
        lastw, readers, waited = {}, {}, {}
        for o in ops:
            stream = o["eng"]
            deps = []
            for r in o["reads"]:
                w = lastw.get(r)
                if w is not None:
                    deps.append((w, True))
            for wkey in o["writes"]:
                w = lastw.get(wkey)
                if w is not None:
                    deps.append((w, False))
                deps.extend((x, False) for x in readers.get(wkey, {}).values())
            waits = {}
            for d, is_raw in deps:
                if d is o:
                    continue
                if (not is_raw) and (not o["dma"]) and (not d["dma"]) and d["eng"] == stream:
                    continue
                kind, sid, val = d["tok"]
                if o["dma"] and d["dma"] and d["key"] == o["key"]:
                    for semid, v in d["allwaits"].items():
                        if waits.get(semid, 0) < v:
                            waits[semid] = v
                    continue
                if kind == "e":
                    if sid == stream and not o["dma"] and sid == "pe":
                        continue
                    semid, v = ("e", sid), val
                else:
                    semid, v = ("k", sid), val * 16
                if waits.get(semid, 0) < v:
                    waits[semid] = v
            o["allwaits"] = dict(waits)
            fw = []
            for semid, v in waits.items():
                if waited.get((stream, semid), 0) >= v:
                    continue
                waited[(stream, semid)] = v
                fw.append((semid, v))
            o["waits"] = fw
            for r in o["reads"]:
                readers.setdefault(r, {})[(o["eng"], o["dma"], o["tok"][1] if o["dma"] else 0)] = o
            for wkey in o["writes"]:
                lastw[wkey] = o
                readers[wkey] = {}
        self.final = [(("e", e), c) for e, c in cnt.items()] + [(("k", k), c * 16) for k, c in kcnt.items()]
        return cnt, kcnt

    def run(self):
        nc = self.nc
        cnt, kcnt = self.emit()
        with contextlib.ExitStack() as st:
            sems = {}
            for e in cnt:
                sems[("e", e)] = st.enter_context(nc.semaphore("se_" + e))
            for i, k in enumerate(kcnt):
                sems[("k", k)] = st.enter_context(nc.semaphore("sk_%d" % i))
            block = st.enter_context(nc.Block())
            ops, final = self.ops, self.final

            def replay(stream, engobj, do_final=False):
                for o in ops:
                    if o["eng"] != stream:
                        continue
                    for semid, v in o["waits"]:
                        engobj.wait_ge(sems[semid], v)
                    ins = o["fn"](engobj)
                    kind, sid, val = o["tok"]
                    if kind == "e":
                        ins.then_inc(sems[("e", sid)], 1)
                    else:
                        ins.then_inc(sems[("k", sid)], 16)
                if do_final:
                    for semid, v in final:
                        engobj.wait_ge(sems[semid], v)

            @block.sync
            def _(e):
                replay("sp", e, do_final=True)

            @block.tensor
            def _(e):
                replay("pe", e)

            @block.scalar
            def _(e):
                replay("act", e)

            @block.vector
            def _(e):
                replay("dve", e)

            @block.gpsimd
            def _(e):
                replay("pool", e)


def build():
    nc = bass.Bass("TRN2", target_bir_lowering=False)
    P = Prog(nc)

    def din(name, shape, dt=F32):
        return nc.dram_tensor(name, list(shape), dt, kind="ExternalInput").ap()

    def dout(name, shape, dt=F32):
        return nc.dram_tensor(name, list(shape), dt, kind="ExternalOutput").ap()

    def dscr(name, shape, dt):
        return nc.dram_tensor(name, list(shape), dt, kind="Internal").ap()

    xp = din("xp", [L, D]); xs = din("xs", [32, D])
    c_all = din("c_all", [NPOOL * 128, 2056])
    pt = din("pt", [1, 256], I32)
    s_lc = din("s_lc", [4, 3, DRNN]); s_lh = din("s_lh", [4, DRNN]); s_fc = din("s_fc", [2, 4, 2, DFF])
    mix_g = din("mix_g", [2, D]); ffn_g = din("ffn_g", [2, D]); fin_g = din("fin_g", [1, D])
    w_in = din("w_in", [D, DIN]); b_f = din("b_f", [1, 8])
    lq1 = din("lq1", [1, 64]); lk1 = din("lk1", [1, 64]); lq2 = din("lq2", [1, 64]); lk2 = din("lk2", [1, 64])
    subg = din("subg", [128, 1]); w_out = din("w_out", [D, D])
    w_gate = din("w_gate", [D, DRNN]); w_x = din("w_x", [D, DRNN])
    lcw = din("lcw", [4, DRNN]); lcb = din("lcb", [1, DRNN])
    lwa = din("lwa", [16, 80, 80]); lba = din("lba", [1, DRNN]); lwi = din("lwi", [16, 80, 80]); lbi = din("lbi", [1, DRNN])
    llam = din("llam", [1, DRNN]); w_lo = din("w_lo", [DRNN, D])
    w_a = din("w_a", [2, D, DFF]); w_b = din("w_b", [2, D, DFF]); fcw = din("fcw", [2, 3, DFF]); fcb = din("fcb", [2, DFF])
    w_dn = din("w_dn", [2, DFF, D])
    k_ident = din("k_ident", [128, 128], BF16); k_tri = din("k_tri", [128, 128]); k_ones = din("k_ones", [128, 128])
    k_cosp = din("k_cosp", [L, 8]); k_sinp = din("k_sinp", [L, 8]); k_coss = din("k_coss", [32, 8]); k_sins = din("k_sins", [32, 8])
    k_piota = din("k_piota", [128, 1])

    y_p = dout("y_p", [L, D]); y_s = dout("y_s", [32, D])
    o_pfk = dout("o_pfk", [L, 512]); o_pfv = dout("o_pfv", [L, 512]); o_pfl = dout("o_pfl", [L, 8])
    o_pdk = dout("o_pdk", [L, 512]); o_pdv = dout("o_pdv", [L, 512])
    o_plc = dout("o_plc", [1, 3, DRNN]); o_plh = dout("o_plh", [1, DRNN]); o_pfc = dout("o_pfc", [2, 1, 2, DFF])
    o_sfk = dout("o_sfk", [32, 512]); o_sfv = dout("o_sfv", [32, 512]); o_sfl = dout("o_sfl", [32, 8])
    o_sdk = dout("o_sdk", [32, 512]); o_sdv = dout("o_sdv", [32, 512])
    o_slc = dout("o_slc", [4, 3, DRNN]); o_slh = dout("o_slh", [4, DRNN]); o_sfc = dout("o_sfc", [2, 4, 2, DFF])

    KTp = dscr("KTp", [1024, L], BF16); QTp = dscr("QTp", [1024, L], BF16); VSp = dscr("VSp", [L, 1024], BF16)
    Cp = dscr("Cp", [L, 8], F32); ATp = dscr("ATp", [1024, L], BF16)
    XAp = dscr("XAp", [L, D], F32); XBp = dscr("XBp", [L, D], F32); ZTp = dscr("ZTp", [DFF, L], BF16)
    KTs = [dscr("KTs%d" % b, [1024, SK], BF16) for b in range(4)]
    VSs = [dscr("VSs%d" % b, [SK, 1024], BF16) for b in range(4)]
    Cs = [dscr("Cs%d" % b, [SK, 8], F32) for b in range(4)]
    QTs = dscr("QTs", [1024, 32], BF16); ATs = dscr("ATs", [1024, 32], BF16)
    XAs = dscr("XAs", [32, D], F32); XBs = dscr("XBs", [32, D], F32); ZTs = dscr("ZTs", [DFF, 32], BF16)
    NEWR = dscr("NEWR", [32, DIN], F32); NEWL = dscr("NEWL", [32, 8], F32)

    st = contextlib.ExitStack()
    with st:
        def sb(name, shape, dt):
            return st.enter_context(nc.sbuf_tensor(name, list(shape), dt))

        def psum(name, shape, dt):
            return st.enter_context(nc.psum_tensor(name, list(shape), dt))

        WA = sb("WA", [128, 25000], BF16); WB = sb("WB", [128, 25000], BF16)
        F0 = sb("F0", [128, DIN], F32)
        X4 = sb("X4", [128, 3, D], F32); H4 = sb("H4", [128, 1, D], BF16); HT = sb("HT", [128, 8, 512], BF16)
        SQ = sb("SQ", [128, D], F32); GB = sb("GB", [128, D], F32)
        ident = sb("ident", [128, 128], BF16); tri = sb("tri", [128, 128], F32); onesf = sb("onesf", [128, 128], F32)
        maskb = sb("maskb", [128, 128], BF16); onesb = sb("onesb", [128, 128], BF16)
        sm = sb("sm", [128, 64], F32)
        bfr = sb("bfr", [128, 8], F32); LF = sb("LF", [128, 8], F32); CR = sb("CR", [128, 8], F32); CT = sb("CT", [128, 8], F32)
        cosT = sb("cosT", [128, 8], F32); sinT = sb("sinT", [128, 8], F32); RT = sb("RT", [128, 2, 8, 8], F32)
        QB = sb("QB", [128, 1024], BF16); KB = sb("KB", [128, 1024], BF16); VB = sb("VB", [128, 1024], BF16)
        TST = sb("TST", [128, 8, 128], BF16)
        PG = [X4[:].rearrange("p a b -> p (a b)")[:, 0:2056], F0[:, 0:2056]]
        PGK = ["X4", "F0"]
        CT2 = sb("CT2", [128, 8], F32)
        PTI = sb("PTI", [128, 256], I32); PTF = sb("PTF", [128, 256], F32); OFF = sb("OFF", [128, 256], I32)
        piota = sb("piota", [128, 1], F32)
        LAMT = sb("LAMT", [128, 4, 64], F32); LAM = sb("LAM", [128, 4], F32); subgT = sb("subgT", [128, 1], F32)
        QTt = sb("QTt", [128, 512], BF16)
        PTab = sb("PTab", [128, 2, 512], BF16); PTbb = sb("PTbb", [128, 2, 512], BF16)
        PTa = [PTab[:, i, :] for i in range(2)]
        PTb = [PTbb[:, i, :] for i in range(2)]
        CTOK = sb("CTOK", [128, 65], F32); CEND = sb("CEND", [128, 32], F32); BIA = sb("BIA", [128, 4, 65], F32)
        OA = sb("OA", [128, 512], F32); OB = sb("OB", [128, 512], F32); OC = sb("OC", [128, 520], F32)
        AOUT = sb("AOUT", [128, 512], BF16)
        ATt = sb("ATt", [128, 24, 128], BF16)
        ABUF = sb("ABUF", [128, 520], F32); T1 = sb("T1", [128, 512], F32); T2 = sb("T2", [128, 512], F32)
        ZST = sb("ZST", [128, 512], BF16)
        HALOp = sb("HALOp", [128, 24, 1, 2], F32); HALOs = sb("HALOs", [128, 24, 4, 2], F32)
        fcwT = sb("fcwT", [128, 3, 24], F32); fcbT = sb("fcbT", [128, 24], F32)
        UHp = sb("UHp", [128, 10, 1, 3], F32); UHs = sb("UHs", [128, 10, 4, 3], F32)
        HSTp = sb("HSTp", [128, 10, 1], F32); HSTs = sb("HSTs", [128, 10, 4], F32)
        lcwT = sb("lcwT", [128, 4, 10], F32); lcbT = sb("lcbT", [128, 10], F32)
        lbaT = sb("lbaT", [128, 10], F32); lbiT = sb("lbiT", [128, 10], F32); lsc = sb("lsc", [128, 10], F32)
        nlbaT = sb("nlbaT", [128, 10], F32); nlbiT = sb("nlbiT", [128, 10], F32)
        GT = sb("GT", [128, 10, 256], BF16); XC = F0[:, 0:2560].rearrange("p (m n) -> p m n", n=256); XCb = sb("XCb", [128, 10, 256], BF16)
        YT = sb("YT", [128, 10, 256], BF16)
        UBUF = sb("UBUF", [128, 520], F32); HS = sb("HS", [128, 512], F32)

        psT = psum("psT", [128, 512], F32)
        psA = psum("psA", [128, 512], F32); psB = psum("psB", [128, 512], F32)
        S1 = psum("S1", [128, 512], F32); S2 = psum("S2", [128, 512], F32)
        O1 = psum("O1", [128, 512], F32); O2 = psum("O2", [128, 512], F32)
        D1 = psum("D1", [128, 512], F32)
        D2 = psB
        psTv = [psT[:].bitcast(BF16), S1[:].bitcast(BF16)]
        psTk = ["psT", "S1"]

        def dma(eng, out, in_, reads=(), writes=(), key=None):
            P.op(eng, lambda e, o=out, i=in_: e.dma_start(out=o, in_=i), reads, writes, dma=True, key=key)

        def dma_nc(eng, out, in_, reads=(), writes=(), key=None):
            P.op(eng, lambda e, o=out, i=in_: e.dma_start(out=o, in_=i, allow_slow_non_contiguous=True),
                 reads, writes, dma=True, key=key)

        def mm(out, lhsT, rhs, start, stop, reads, writes):
            P.op("pe", lambda e, o=out, l=lhsT, r=rhs, s=start, t=stop: e.matmul(o, lhsT=l, rhs=r, start=s, stop=t),
                 reads, writes)

        def tr(out, in_, idn, reads, writes):
            P.op("pe", lambda e, o=out, i=in_, d=idn: e.transpose(out=o, in_=i, identity=d), reads, writes)

        def act(out, in_, func, reads, writes, bias=None, scale=None):
            def f(e, o=out, i=in_, fn=func, b=bias, s=scale):
                kw = {}
                if b is not None:
                    kw["bias"] = b
                if s is not None:
                    kw["scale"] = s
                return e.activation(out=o, in_=i, func=fn, **kw)
            P.op("act", f, reads, writes)

        def cp(eng, out, in_, reads, writes):
            if eng == "act":
                P.op("act", lambda e, o=out, i=in_: e.copy(out=o, in_=i), reads, writes)
            else:
                P.op(eng, lambda e, o=out, i=in_: e.tensor_copy(out=o, in_=i), reads, writes)

        def tt(eng, out, in0, in1, op, reads, writes):
            P.op(eng, lambda e, o=out, a=in0, b=in1, p=op: e.tensor_tensor(out=o, in0=a, in1=b, op=p), reads, writes)

        def ts(eng, out, in0, s1, s2, op0, op1, reads, writes):
            if s2 is None:
                P.op(eng, lambda e, o=out, a=in0, x=s1, p=op0: e.tensor_scalar(out=o, in0=a, scalar1=x, scalar2=None, op0=p),
                     reads, writes)
            else:
                P.op(eng, lambda e, o=out, a=in0, x=s1, y=s2, p=op0, q=op1:
                     e.tensor_scalar(out=o, in0=a, scalar1=x, scalar2=y, op0=p, op1=q), reads, writes)

        def stt(eng, out, in0, scalar, in1, op0, op1, reads, writes):
            P.op(eng, lambda e, o=out, a=in0, s=scalar, b=in1, p=op0, q=op1:
                 e.scalar_tensor_tensor(out=o, in0=a, scalar=s, in1=b, op0=p, op1=q), reads, writes)

        def memset(eng, ap, val, writes):
            P.op(eng, lambda e, a=ap, v=val: e.memset(a, v), (), writes)

        def rsum(out, in_, reads, writes):
            P.op("dve", lambda e, o=out, i=in_: e.reduce_sum(out=o, in_=i, axis=AX.X), reads, writes)

        def recip(out, in_, reads, writes):
            P.op("dve", lambda e, o=out, i=in_: e.reciprocal(out=o, in_=i), reads, writes)

        def wload(dst, src, rkey):
            dma("pool", dst, src, (), [rkey])

        dma("sp", ident[:], k_ident, (), ["ident"]); dma("sp", tri[:], k_tri, (), ["tri"])
        dma("sp", onesf[:], k_ones, (), ["onesf"]); dma("sp", piota[:], k_piota, (), ["piota"])
        cp("dve", maskb[:], tri[:], ["tri"], ["maskb"]); cp("dve", onesb[:], onesf[:], ["onesf"], ["onesb"])
        dma("sp", bfr[:], b_f.partition_broadcast(128), (), ["bfr"])
        dma("sp", subgT[:], subg, (), ["subgT"])
        for i, v in enumerate((lq1, lk1, lq2, lk2)):
            dma("sp", LAMT[:, i, :], v.partition_broadcast(128), (), ["LAMT"], key="ld_lamt")
        tt("dve", LAMT[:, 0, :], LAMT[:, 0, :], LAMT[:, 1, :], ALU.mult, ["LAMT"], ["LAMT"])
        tt("dve", LAMT[:, 2, :], LAMT[:, 2, :], LAMT[:, 3, :], ALU.mult, ["LAMT"], ["LAMT"])
        rsum(LAM[:, 0:1], LAMT[:, 0, :], ["LAMT"], ["LAM"]); rsum(LAM[:, 1:2], LAMT[:, 2, :], ["LAMT"], ["LAM"])
        act(LAM[:, 0:2], LAM[:, 0:2], AF.Exp, ["LAM"], ["LAM"])
        tt("dve", LAM[:, 2:3], LAM[:, 0:1], LAM[:, 1:2], ALU.subtract, ["LAM"], ["LAM"])
        ts("dve", LAM[:, 3:4], LAM[:, 2:3], LAM_INIT0, -1.0, ALU.add, ALU.mult, ["LAM"], ["LAM"])
        dma("sp", PTI[:], pt.partition_broadcast(128), (), ["PTI"])
        cp("dve", PTF[:], PTI[:], ["PTI"], ["PTF"])
        ts("dve", PTF[:], PTF[:], 128.0, piota[:, 0:1], ALU.mult, ALU.add, ["PTF", "piota"], ["PTF"])
        cp("dve", OFF[:], PTF[:], ["PTF"], ["OFF"])

        def load_gamma(src_row):
            dma("sp", GB[:], src_row.partition_broadcast(128), (), ["GB"])

        def rmsnorm(xt, ht, rows):
            tt("dve", SQ[0:rows, :], xt, xt, ALU.mult, ["X4"], ["SQ"])
            rsum(sm[0:rows, 0:1], SQ[0:rows, :], ["SQ"], ["sm"])
            ts("dve", sm[0:rows, 0:1], sm[0:rows, 0:1], 1.0 / D, EPS, ALU.mult, ALU.add, ["sm"], ["sm"])
            act(sm[0:rows, 1:2], sm[0:rows, 0:1], AF.Ln, ["sm"], ["sm"])
            act(sm[0:rows, 2:3], sm[0:rows, 1:2], AF.Exp, ["sm"], ["sm"], scale=-0.5)
            stt("dve", ht, xt, sm[0:rows, 2:3], GB[0:rows, :], ALU.mult, ALU.mult, ["X4", "sm", "GB"], ["H4"])

        def transpose_to(src_bf, rows, dst, dkey, rkey, par=0):
            pv = psTv[par]; pk = psTk[par]
            rk = list(rkey) if isinstance(rkey, (list, tuple)) else [rkey]
            dk = list(dkey) if isinstance(dkey, (list, tuple)) else [dkey]
            for k in range(8):
                tr(pv[:, k * 128:k * 128 + rows], src_bf[:, k * 128:(k + 1) * 128], ident[0:rows, 0:rows],
                   rk + ["ident"], [pk])
            cp("act", dst, pv.rearrange("p (k r) -> p k r", k=8)[:, :, 0:rows], [pk], dk)

        KBs = [KB[:, :], PTab[:].rearrange("p a b -> p (a b)")]
        KBk = [["KB"], ["PTa0", "PTa1"]]
        VBs = [VB[:, :], PTbb[:].rearrange("p a b -> p (a b)")]
        VBk = [["VB"], ["PTb0", "PTb1"]]
        TSTs = [TST[:, :, :], GT[:].rearrange("p a b -> p (a b)")[:, 0:1024].rearrange("p (j r) -> p j r", j=8)]
        TSTk = [["TST"], ["GT"]]
        CTs = [CT, CT2]
        CTk = ["CT", "CT2"]
        CPS = [(psA, "psA", psB, "psB"), (O1, "O1", O2, "O2")]

        def ingest(ktok, vtok, kdtok, vdtok, lft, rkeys, KT, VS, C, t, ckey, par=0):
            kb = KBs[par]; kbk = KBk[par]; vb = VBs[par]; vbk = VBk[par]; tst = TSTs[par]; tstk = TSTk[par]
            pa, pak, pb_, pbk = CPS[par]
            cp("act", kb[:, 0:512], ktok, rkeys, kbk); cp("dve", kb[:, 512:1024], kdtok, rkeys, kbk)
            transpose_to(kb, 128, tst, tstk, kbk, par)
            dma("sp", KT[:, t * 128:(t + 1) * 128].rearrange("(j p) r -> p j r", p=128), tst, tstk, [ckey + "KT"], key="st_TST%d" % par)
            cp("act", vb[:, 0:512], vtok, rkeys, vbk); cp("dve", vb[:, 512:1024], vdtok, rkeys, vbk)
            dma("sp", VS[t * 128:(t + 1) * 128, :], vb, vbk, [ckey + "VS"], key="st_VB%d" % par)
            mm(pa[:, 0:8], tri[:, :], lft, True, True, ["tri"] + list(rkeys), [pak])
            mm(pb_[:, 0:8], onesf[:, :], lft, True, True, ["onesf"] + list(rkeys), [pbk])
            tt("dve", CTs[par][:, :], pa[:, 0:8], CR[:, :], ALU.add, [pak, "CR"], [CTk[par]])
            dma("sp", C[t * 128:(t + 1) * 128, :], CTs[par][:, :], [CTk[par]], [ckey + "C"], key="st_CT%d" % par)
            tt("dve", CR[:, :], pb_[:, 0:8], CR[:, :], ALU.add, [pbk, "CR"], ["CR"])

        for k in range(8):
            wload(WA[:, k * DIN:(k + 1) * DIN], w_in[k * 128:(k + 1) * 128, :], "WA")
        load_gamma(mix_g[0:1, :])
        memset("pool", CR[:, :], 0.0, ["CR"])

        def inproj_tile(xsrc, rows, cos_src, sin_src):
            dma("sp", X4[0:rows, 0, :], xsrc, (), ["X4"])
            rmsnorm(X4[0:rows, 0, :], H4[0:rows, 0, :], rows)
            transpose_to(H4[0:rows, 0, :], rows, HT[:, :, 0:rows], "HT", "H4")
            for p in range(7):
                c0 = p * 512
                w = min(512, DIN - c0)
                ps = psA if p % 2 == 0 else psB
                pk = "psA" if p % 2 == 0 else "psB"
                for k in range(8):
                    mm(ps[0:rows, 0:w], HT[:, k, 0:rows], WA[:, k * DIN + c0:k * DIN + c0 + w], k == 0, k == 7,
                       ["HT", "WA"], [pk])
                cp("act" if p % 2 == 0 else "dve", F0[0:rows, c0:c0 + w], ps[0:rows, 0:w], [pk], ["F0"])
            tt("dve", LF[0:rows, :], F0[0:rows, 1536:1544], bfr[0:rows, :], ALU.add, ["F0", "bfr"], ["LF"])
            act(LF[0:rows, :], LF[0:rows, :], AF.Exp, ["LF"], ["LF"], scale=-1.0)
            act(LF[0:rows, :], LF[0:rows, :], AF.Ln, ["LF"], ["LF"], bias=1.0)
            ts("dve", LF[0:rows, :], LF[0:rows, :], -1.0, None, ALU.mult, ALU.bypass, ["LF"], ["LF"])
            dma("sp", cosT[0:rows, :], cos_src, (), ["cosT"]); dma("sp", sinT[0:rows, :], sin_src, (), ["sinT"])
            cb = cosT[0:rows, :].unsqueeze(1).broadcast_to([rows, 8, 8])
            sbb = sinT[0:rows, :].unsqueeze(1).broadcast_to([rows, 8, 8])
            for base in (1544, 2056):
                v = F0[0:rows, base:base + 512].rearrange("p (g d) -> p g d", d=64)
                x1 = v[:, :, 0:8]; x2 = v[:, :, 8:16]
                a1 = RT[0:rows, 0, :, :]; a2 = RT[0:rows, 1, :, :]
                tt("dve", a1, x1, sbb, ALU.mult, ["F0", "sinT"], ["RT"])
                tt("dve", a2, x2, sbb, ALU.mult, ["F0", "sinT"], ["RT"])
                tt("dve", x1, x1, cb, ALU.mult, ["F0", "cosT"], ["F0"])
                tt("dve", x2, x2, cb, ALU.mult, ["F0", "cosT"], ["F0"])
                tt("dve", x1, x1, a2, ALU.subtract, ["F0", "RT"], ["F0"])
                tt("dve", x2, x2, a1, ALU.add, ["F0", "RT"], ["F0"])
            cp("pool", QB[0:rows, 0:512], F0[0:rows, 0:512], ["F0"], ["QB"])
            cp("dve", QB[0:rows, 512:1024], F0[0:rows, 1544:2056], ["F0"], ["QB"])

        for t in range(NTP):
            r0 = t * 128
            inproj_tile(xp[r0:r0 + 128, :], 128, k_cosp[r0:r0 + 128, :], k_sinp[r0:r0 + 128, :])
            transpose_to(QB[:, :], 128, TST[:, :, :], "TST", "QB")
            dma("sp", QTp[:, r0:r0 + 128].rearrange("(j p) r -> p j r", p=128), TST[:, :, :], ["TST"], ["QTp"], key="st_TST")
            dma("sp", o_pfk[r0:r0 + 128, :], F0[:, 512:1024], ["F0"], (), key="st_F0")
            dma("sp", o_pfv[r0:r0 + 128, :], F0[:, 1024:1536], ["F0"], (), key="st_F0")
            dma("sp", o_pdk[r0:r0 + 128, :], F0[:, 2056:2568], ["F0"], (), key="st_F0")
            dma("sp", o_pdv[r0:r0 + 128, :], F0[:, 2568:3080], ["F0"], (), key="st_F0")
            dma("sp", o_pfl[r0:r0 + 128, :], LF[:, :], ["LF"], (), key="st_LF")
            ingest(F0[:, 512:1024], F0[:, 1024:1536], F0[:, 2056:2568], F0[:, 2568:3080], LF[:, :], ["F0", "LF"],
                   KTp, VSp, Cp, t, "p", t % 2)

        inproj_tile(xs[:, :], 32, k_coss, k_sins)
        transpose_to(QB[0:32, :], 32, TST[:, :, 0:32], "TST", "QB")
        dma("sp", QTs.rearrange("(j p) r -> p j r", p=128), TST[:, :, 0:32], ["TST"], ["QTs"], key="st_TST")
        dma("sp", o_sfk, F0[0:32, 512:1024], ["F0"], (), key="st_F0")
        dma("sp", o_sfv, F0[0:32, 1024:1536], ["F0"], (), key="st_F0")
        dma("sp", o_sdk, F0[0:32, 2056:2568], ["F0"], (), key="st_F0")
        dma("sp", o_sdv, F0[0:32, 2568:3080], ["F0"], (), key="st_F0")
        dma("sp", o_sfl, LF[0:32, :], ["LF"], (), key="st_LF")
        dma("sp", NEWR, F0[0:32, :], ["F0"], ["NEWR"], key="st_F0")
        dma("sp", NEWL, LF[0:32, :], ["LF"], ["NEWL"], key="st_LF")

        for b in range(4):
            memset("pool", CR[:, :], 0.0, ["CR"])
            for j in range(65):
                pb = j % 2
                pg = PG[pb]
                pk = PGK[pb]
                if j < 64:
                    col = b * 64 + j
                    P.op("pool", lambda e, o=pg, c=col: e.indirect_dma_start(
                        out=o, out_offset=None, in_=c_all, in_offset=bass.IndirectOffsetOnAxis(ap=OFF[:, c:c + 1], axis=0)),
                        ["OFF"], [pk], dma=True, key="ld_" + pk)
                else:
                    memset("pool", pg, 0.0, [pk])
                    rr = slice(b * 8, b * 8 + 8)
                    dma("sp", pg[0:8, 0:512], NEWR[rr, 512:1024], ["NEWR"], [pk], key="ld_" + pk)
                    dma("sp", pg[0:8, 512:1024], NEWR[rr, 1024:1536], ["NEWR"], [pk], key="ld_" + pk)
                    dma("sp", pg[0:8, 1024:1032], NEWL[rr, :], ["NEWL"], [pk], key="ld_" + pk)
                    dma("sp", pg[0:8, 1032:1544], NEWR[rr, 2056:2568], ["NEWR"], [pk], key="ld_" + pk)
                    dma("sp", pg[0:8, 1544:2056], NEWR[rr, 2568:3080], ["NEWR"], [pk], key="ld_" + pk)
                ingest(pg[:, 0:512], pg[:, 512:1024], pg[:, 1032:1544], pg[:, 1544:2056], pg[:, 1024:1032], [pk],
                       KTs[b], VSs[b], Cs[b], j, "s%d" % b, pb)

        VA = WB[:, 0:65 * 128].rearrange("p (t c) -> p t c", c=128)
        KTt = WA

        def vload(VS, r0c, ncol, n_kt, ckey):
            for t0 in range(0, n_kt, 16):
                t1 = min(n_kt, t0 + 16)
                dma_nc("sp", VA[:, t0:t1, 0:ncol], VS[t0 * 128:t1 * 128, r0c:r0c + ncol].rearrange("(t p) c -> p t c", p=128),
                       [ckey + "VS"], ["WB"], key="ld_VA")

        def attend(KT, VS, C, ckey, n_kt_all, qtiles):
            nk_all = n_kt_all * 128

            def active_c0(subs, nq, kt):
                c0 = nq
                for (cc, ncol, dg, _) in subs:
                    if dg >= kt:
                        c0 = min(c0, cc)
                return c0

            memset("pool", VA[:, 0:n_kt_all, 64:128], 1.0, ["WB"])
            for h in range(8):
                dma("sp", KTt[0:64, 0:nk_all], KT[h * 64:(h + 1) * 64, 0:nk_all], [ckey + "KT"], ["WA"])
                vload(VS, h * 64, 64, n_kt_all, ckey)
                dma_nc("sp", CTOK[:, 0:n_kt_all], C[0:nk_all, h:h + 1].rearrange("(t p) o -> p (t o)", p=128), [ckey + "C"], ["CTOK"])
                for (qsrc, qkey, nq, subs, n_kt, atdst, atkey) in qtiles:
                    dma("sp", QTt[0:64, 0:nq], qsrc[h * 64:(h + 1) * 64, :], [qkey], ["QTt"])
                    for si, (cc, ncol, dg, cpos) in enumerate(subs):
                        dma("sp", CEND[:, si:si + 1], C[cpos:cpos + 1, h:h + 1].partition_broadcast(128), [ckey + "C"], ["CEND"],
                            key="ld_cend")
                    for si in range(len(subs)):
                        ts("dve", BIA[:, si, 0:n_kt], CTOK[:, 0:n_kt], CEND[:, si:si + 1], -1.0, ALU.subtract, ALU.mult,
                           ["CTOK", "CEND"], ["BIA"])
                    SB4 = [(S1, "S1"), (S2, "S2"), (O2, "O2"), (D1, "D1")]
                    PB4 = [(PTa[0], "PTa0"), (PTa[1], "PTa1"), (PTb[0], "PTb0"), (PTb[1], "PTb1")]
                    kts = [kt for kt in range(n_kt) if active_c0(subs, nq, kt) < nq]

                    def fox_qk(kt):
                        c0 = active_c0(subs, nq, kt)
                        S, skey = SB4[kt % 4]
                        mm(S[:, c0:nq], KTt[0:64, kt * 128:(kt + 1) * 128], QTt[0:64, c0:nq], True, True, ["WA", "QTt"], [skey])

                    LOOK = 2
                    for kk in kts[:LOOK]:
                        fox_qk(kk)
                    for ki, kt in enumerate(kts):
                        if ki + LOOK < len(kts):
                            fox_qk(kts[ki + LOOK])
                        c0 = active_c0(subs, nq, kt)
                        S, skey = SB4[kt % 4]
                        PTt, pkey = PB4[kt % 4]
                        for si, (cc, ncol, dg, _) in enumerate(subs):
                            if dg < kt:
                                continue
                            act(PTt[:, cc:cc + ncol], S[:, cc:cc + ncol], AF.Exp, [skey, "BIA"], [pkey],
                                bias=BIA[:, si, kt:kt + 1], scale=0.125)
                            if dg == kt:
                                tt("pool", PTt[:, cc:cc + ncol], PTt[:, cc:cc + ncol], maskb[:, 0:ncol], ALU.mult,
                                   [pkey, "maskb"], [pkey])
                        mm(O1[:, c0:nq], VA[:, kt, :], PTt[:, c0:nq], kt == 0, kt == n_kt - 1, ["WB", pkey], ["O1"])
                    recip(OA[64:128, 0:nq], O1[64:128, 0:nq], ["O1"], ["OA"])
                    cp("dve", OB[0:64, 0:nq], OA[64:128, 0:nq], ["OA"], ["OB"])
                    tt("dve", AOUT[0:64, 0:nq], O1[0:64, 0:nq], OB[0:64, 0:nq], ALU.mult, ["O1", "OB"], ["AOUT"])
                    dma("sp", atdst[h * 64:(h + 1) * 64, :], AOUT[0:64, 0:nq], ["AOUT"], [atkey], key="st_AOUT")
            for hd in range(4):
                r0 = 512 + hd * 128
                dma("sp", KTt[:, 0:nk_all], KT[r0:r0 + 128, 0:nk_all], [ckey + "KT"], ["WA"])
                vload(VS, r0, 128, n_kt_all, ckey)
                for (qsrc, qkey, nq, subs, n_kt, atdst, atkey) in qtiles:
                    dma("sp", QTt[:, 0:nq], qsrc[r0:r0 + 128, :], [qkey], ["QTt"])
                    SP2 = [((S1, "S1"), (S2, "S2")), ((psA, "psA"), (psT, "psT"))]
                    kts = [kt for kt in range(n_kt) if active_c0(subs, nq, kt) < nq]

                    def diff_qk(kt):
                        c0 = active_c0(subs, nq, kt)
                        ks = slice(kt * 128, (kt + 1) * 128)
                        (Sa, sak), (Sb, sbk) = SP2[kt % 2]
                        mm(Sa[:, c0:nq], KTt[0:64, ks], QTt[0:64, c0:nq], True, True, ["WA", "QTt"], [sak])
                        mm(Sb[:, c0:nq], KTt[64:128, ks], QTt[64:128, c0:nq], True, True, ["WA", "QTt"], [sbk])

                    diff_qk(kts[0])
                    for ki, kt in enumerate(kts):
                        if ki + 1 < len(kts):
                            diff_qk(kts[ki + 1])
                        c0 = active_c0(subs, nq, kt)
                        Pa = PTa[kt % 2]; Pb = PTb[kt % 2]
                        pka = "PTa%d" % (kt % 2); pkb = "PTb%d" % (kt % 2)
                        (Sa, sak), (Sb, sbk) = SP2[kt % 2]
                        act(Pa[:, c0:nq], Sa[:, c0:nq], AF.Exp, [sak], [pka], scale=0.125)
                        act(Pb[:, c0:nq], Sb[:, c0:nq], AF.Exp, [sbk], [pkb], scale=0.125)
                        for si, (cc, ncol, dg, _) in enumerate(subs):
                            if dg == kt:
                                tt("pool", Pa[:, cc:cc + ncol], Pa[:, cc:cc + ncol], maskb[:, 0:ncol], ALU.mult, [pka, "maskb"], [pka])
                                tt("dve", Pb[:, cc:cc + ncol], Pb[:, cc:cc + ncol], maskb[:, 0:ncol], ALU.mult, [pkb, "maskb"], [pkb])
                        first = kt == 0
                        last = kt == n_kt - 1
                        mm(O1[:, c0:nq], VA[:, kt, :], Pa[:, c0:nq], first, last, ["WB", pka], ["O1"])
                        mm(O2[:, c0:nq], VA[:, kt, :], Pb[:, c0:nq], first, last, ["WB", pkb], ["O2"])
                        mm(D1[:, c0:nq], onesb[:, :], Pa[:, c0:nq], first, last, ["onesb", pka], ["D1"])
                        mm(D2[:, c0:nq], onesb[:, :], Pb[:, c0:nq], first, last, ["onesb", pkb], ["psB"])
                    recip(OA[:, 0:nq], D1[:, 0:nq], ["D1"], ["OA"])
                    recip(OB[:, 0:nq], D2[:, 0:nq], ["psB"], ["OB"])
                    tt("dve", OA[:, 0:nq], OA[:, 0:nq], O1[:, 0:nq], ALU.mult, ["OA", "O1"], ["OA"])
                    tt("dve", OB[:, 0:nq], OB[:, 0:nq], O2[:, 0:nq], ALU.mult, ["OB", "O2"], ["OB"])
                    stt("dve", OA[:, 0:nq], OB[:, 0:nq], LAM[:, 3:4], OA[:, 0:nq], ALU.mult, ALU.add, ["OA", "OB", "LAM"], ["OA"])
                    tt("dve", OC[:, 0:nq], OA[:, 0:nq], OA[:, 0:nq], ALU.mult, ["OA"], ["OC"])
                    mm(psA[:, 0:nq], onesf[:, :], OC[:, 0:nq], True, True, ["onesf", "OC"], ["psA"])
                    ts("dve", OC[:, 0:nq], psA[:, 0:nq], 1.0 / 128.0, EPS, ALU.mult, ALU.add, ["psA"], ["OC"])
                    act(OC[:, 0:nq], OC[:, 0:nq], AF.Ln, ["OC"], ["OC"])
                    act(OC[:, 0:nq], OC[:, 0:nq], AF.Exp, ["OC"], ["OC"], scale=-0.5)
                    tt("dve", OA[:, 0:nq], OA[:, 0:nq], OC[:, 0:nq], ALU.mult, ["OA", "OC"], ["OA"])
                    ts("dve", AOUT[:, 0:nq], OA[:, 0:nq], subgT[:, 0:1], 1.0 - LAM_INIT0, ALU.mult, ALU.mult, ["OA", "subgT"], ["AOUT"])
                    dma("sp", atdst[r0:r0 + 128, :], AOUT[:, 0:nq], ["AOUT"], [atkey], key="st_AOUT")

        def attend_small(KT, VS, C, ckey, qsrc, qkey, atdst, atkey):
            NKT = 65
            nk_all = NKT * 128
            cpos = 64 * 128 + 7
            v3 = lambda ap: ap.rearrange("p (t q) -> p t q", q=8)
            memset("pool", VA[:, 0:NKT, 64:128], 1.0, ["WB"])
            for h in range(8):
                dma("sp", KTt[0:64, 0:nk_all], KT[h * 64:(h + 1) * 64, 0:nk_all], [ckey + "KT"], ["WA"])
                vload(VS, h * 64, 64, NKT, ckey)
                dma_nc("sp", CTOK[:, 0:NKT], C[0:nk_all, h:h + 1].rearrange("(t p) o -> p (t o)", p=128), [ckey + "C"], ["CTOK"])
                dma("sp", QTt[0:64, 0:8], qsrc[h * 64:(h + 1) * 64, :], [qkey], ["QTt"])
                dma("sp", CEND[:, 0:1], C[cpos:cpos + 1, h:h + 1].partition_broadcast(128), [ckey + "C"], ["CEND"], key="ld_cend")
                ts("dve", BIA[:, 0, 0:NKT], CTOK[:, 0:NKT], CEND[:, 0:1], -8.0, ALU.subtract, ALU.mult, ["CTOK", "CEND"], ["BIA"])
                for kt in range(64):
                    mm(S1[:, kt * 8:(kt + 1) * 8], KTt[0:64, kt * 128:(kt + 1) * 128], QTt[0:64, 0:8], True, True, ["WA", "QTt"], ["S1"])
                mm(S2[:, 0:8], KTt[0:64, 64 * 128:65 * 128], QTt[0:64, 0:8], True, True, ["WA", "QTt"], ["S2"])
                tt("dve", v3(T1[:, 0:512]), v3(S1[:, 0:512]), BIA[:, 0, 0:64].unsqueeze(2).broadcast_to([128, 64, 8]), ALU.add,
                   ["S1", "BIA"], ["T1"])
                ts("dve", T2[:, 0:8], S2[:, 0:8], BIA[:, 0, 64:65], None, ALU.add, ALU.bypass, ["S2", "BIA"], ["T2"])
                act(PTa[0][:, 0:512], T1[:, 0:512], AF.Exp, ["T1"], ["PTa0"], scale=0.125)
                act(PTa[1][:, 0:8], T2[:, 0:8], AF.Exp, ["T2"], ["PTa1"], scale=0.125)
                tt("pool", PTa[1][:, 0:8], PTa[1][:, 0:8], maskb[:, 0:8], ALU.mult, ["PTa1", "maskb"], ["PTa1"])
                for kt in range(64):
                    mm(O1[:, 0:8], VA[:, kt, :], PTa[0][:, kt * 8:(kt + 1) * 8], kt == 0, False, ["WB", "PTa0"], ["O1"])
                mm(O1[:, 0:8], VA[:, 64, :], PTa[1][:, 0:8], False, True, ["WB", "PTa1"], ["O1"])
                recip(OA[64:128, 0:8], O1[64:128, 0:8], ["O1"], ["OA"])
                cp("dve", OB[0:64, 0:8], OA[64:128, 0:8], ["OA"], ["OB"])
                tt("dve", AOUT[0:64, 0:8], O1[0:64, 0:8], OB[0:64, 0:8], ALU.mult, ["O1", "OB"], ["AOUT"])
                dma("sp", atdst[h * 64:(h + 1) * 64, :], AOUT[0:64, 0:8], ["AOUT"], [atkey], key="st_AOUT")
            for hd in range(4):
                r0 = 512 + hd * 128
                dma("sp", KTt[:, 0:nk_all], KT[r0:r0 + 128, 0:nk_all], [ckey + "KT"], ["WA"])
                vload(VS, r0, 128, NKT, ckey)
                dma("sp", QTt[:, 0:8], qsrc[r0:r0 + 128, :], [qkey], ["QTt"])
                for kt in range(64):
                    ks = slice(kt * 128, (kt + 1) * 128)
                    mm(S1[:, kt * 8:(kt + 1) * 8], KTt[0:64, ks], QTt[0:64, 0:8], True, True, ["WA", "QTt"], ["S1"])
                    mm(psA[:, kt * 8:(kt + 1) * 8], KTt[64:128, ks], QTt[64:128, 0:8], True, True, ["WA", "QTt"], ["psA"])
                ks = slice(64 * 128, 65 * 128)
                mm(S2[:, 0:8], KTt[0:64, ks], QTt[0:64, 0:8], True, True, ["WA", "QTt"], ["S2"])
                mm(S2[:, 8:16], KTt[64:128, ks], QTt[64:128, 0:8], True, True, ["WA", "QTt"], ["S2"])
                act(PTa[0][:, 0:512], S1[:, 0:512], AF.Exp, ["S1"], ["PTa0"], scale=0.125)
                act(PTb[0][:, 0:512], psA[:, 0:512], AF.Exp, ["psA"], ["PTb0"], scale=0.125)
                act(PTa[1][:, 0:8], S2[:, 0:8], AF.Exp, ["S2"], ["PTa1"], scale=0.125)
                act(PTb[1][:, 0:8], S2[:, 8:16], AF.Exp, ["S2"], ["PTb1"], scale=0.125)
                tt("pool", PTa[1][:, 0:8], PTa[1][:, 0:8], maskb[:, 0:8], ALU.mult, ["PTa1", "maskb"], ["PTa1"])
                tt("pool", PTb[1][:, 0:8], PTb[1][:, 0:8], maskb[:, 0:8], ALU.mult, ["PTb1", "maskb"], ["PTb1"])
                rsum(OC[:, 0:8], PTa[0][:, 0:512].rearrange("p (t q) -> p q t", q=8), ["PTa0"], ["OC"])
                rsum(OC[:, 8:16], PTb[0][:, 0:512].rearrange("p (t q) -> p q t", q=8), ["PTb0"], ["OC"])
                tt("dve", OC[:, 0:8], OC[:, 0:8], PTa[1][:, 0:8], ALU.add, ["OC", "PTa1"], ["OC"])
                tt("dve", OC[:, 8:16], OC[:, 8:16], PTb[1][:, 0:8], ALU.add, ["OC", "PTb1"], ["OC"])
                mm(D1[:, 0:16], onesf[:, :], OC[:, 0:16], True, True, ["onesf", "OC"], ["D1"])
                for kt in range(64):
                    mm(O1[:, 0:8], VA[:, kt, :], PTa[0][:, kt * 8:(kt + 1) * 8], kt == 0, False, ["WB", "PTa0"], ["O1"])
                mm(O1[:, 0:8], VA[:, 64, :], PTa[1][:, 0:8], False, True, ["WB", "PTa1"], ["O1"])
                for kt in range(64):
                    mm(O2[:, 0:8], VA[:, kt, :], PTb[0][:, kt * 8:(kt + 1) * 8], kt == 0, False, ["WB", "PTb0"], ["O2"])
                mm(O2[:, 0:8], VA[:, 64, :], PTb[1][:, 0:8], False, True, ["WB", "PTb1"], ["O2"])
                nq = 8
                recip(OA[:, 0:nq], D1[:, 0:8], ["D1"], ["OA"])
                recip(OB[:, 0:nq], D1[:, 8:16], ["D1"], ["OB"])
                tt("dve", OA[:, 0:nq], OA[:, 0:nq], O1[:, 0:nq], ALU.mult, ["OA", "O1"], ["OA"])
                tt("dve", OB[:, 0:nq], OB[:, 0:nq], O2[:, 0:nq], ALU.mult, ["OB", "O2"], ["OB"])
                stt("dve", OA[:, 0:nq], OB[:, 0:nq], LAM[:, 3:4], OA[:, 0:nq], ALU.mult, ALU.add, ["OA", "OB", "LAM"], ["OA"])
                tt("dve", OC[:, 16:24], OA[:, 0:nq], OA[:, 0:nq], ALU.mult, ["OA"], ["OC"])
                mm(psB[:, 0:nq], onesf[:, :], OC[:, 16:24], True, True, ["onesf", "OC"], ["psB"])
                ts("dve", OC[:, 16:24], psB[:, 0:nq], 1.0 / 128.0, EPS, ALU.mult, ALU.add, ["psB"], ["OC"])
                act(OC[:, 16:24], OC[:, 16:24], AF.Ln, ["OC"], ["OC"])
                act(OC[:, 16:24], OC[:, 16:24], AF.Exp, ["OC"], ["OC"], scale=-0.5)
                tt("dve", OA[:, 0:nq], OA[:, 0:nq], OC[:, 16:24], ALU.mult, ["OA", "OC"], ["OA"])
                ts("dve", AOUT[:, 0:nq], OA[:, 0:nq], subgT[:, 0:1], 1.0 - LAM_INIT0, ALU.mult, ALU.mult, ["OA", "subgT"], ["AOUT"])
                dma("sp", atdst[r0:r0 + 128, :], AOUT[:, 0:nq], ["AOUT"], [atkey], key="st_AOUT")

        qts = []
        for j in range(8):
            subs = [(sbi * 128, 128, 4 * j + sbi, (4 * j + sbi) * 128 + 127) for sbi in range(4)]
            qts.append((QTp[:, j * 512:(j + 1) * 512], "QTp", 512, subs, 4 * j + 4, ATp[:, j * 512:(j + 1) * 512], "ATp"))
        attend(KTp, VSp, Cp, "p", 32, qts)
        for b in range(4):
            attend_small(KTs[b], VSs[b], Cs[b], "s%d" % b, QTs[:, b * 8:(b + 1) * 8], "QTs", ATs[:, b * 8:(b + 1) * 8], "ATs")

        WO = WA[:, 0:8192].rearrange("p (k n) -> p k n", n=1024)
        for k in range(8):
            wload(WO[:, k, :], w_out[k * 128:(k + 1) * 128, :], "WA")

        def outproj_tile(atsrc, rows, xsrc, xdst, xdkey):
            dma("sp", ATt[:, 0:8, 0:rows], atsrc.rearrange("(k p) r -> p k r", p=128), ["ATp", "ATs"], ["ATt"])
            dma("sp", X4[0:rows, 0, :], xsrc, (), ["X4"])
            for pn in range(2):
                for k in range(8):
                    mm(psA[0:rows, :], ATt[:, k, 0:rows], WO[:, k, pn * 512:(pn + 1) * 512], k == 0, k == 7, ["ATt", "WA"], ["psA"])
                tt("dve", X4[0:rows, 1, pn * 512:(pn + 1) * 512], psA[0:rows, :], X4[0:rows, 0, pn * 512:(pn + 1) * 512], ALU.add,
                   ["psA", "X4"], ["X4"])
            dma("sp", xdst, X4[0:rows, 1, :], ["X4"], [xdkey], key="st_X4")

        for t in range(NTP):
            r0 = t * 128
            outproj_tile(ATp[:, r0:r0 + 128], 128, xp[r0:r0 + 128, :], XAp[r0:r0 + 128, :], "XAp")
        outproj_tile(ATs[:, :], 32, xs[:, :], XAs[:, :], "XAs")

        def ffn(l, srcp, srcs, skeyp, skeys, dstp, dsts, dkeyp, dkeys, final):
            WAa = WA[:, 0:8 * DFF].rearrange("p (k n) -> p k n", n=DFF)
            WBb = WB[:, 0:8 * DFF].rearrange("p (k n) -> p k n", n=DFF)
            for k in range(8):
                wload(WAa[:, k, :], w_a[l, k * 128:(k + 1) * 128, :], "WA")
                wload(WBb[:, k, :], w_b[l, k * 128:(k + 1) * 128, :], "WB")
            load_gamma(ffn_g[l:l + 1, :])
            for j in range(3):
                dma_nc("sp", fcwT[:, j, :], fcw[l, j].rearrange("(f p) -> p f", p=128), (), ["fcwT"], key="ld_fcw")
            dma_nc("sp", fcbT[:, :], fcb[l].rearrange("(f p) -> p f", p=128), (), ["fcbT"])
            memset("pool", HALOp[:], 0.0, ["HALOp"])
            for b in range(4):
                for j in range(2):
                    dma_nc("sp", HALOs[:, :, b, j], s_fc[l, b, j].rearrange("(f p) -> p f", p=128), (), ["HALOs"], key="ld_halo")

            def up_group(src, skey, ntile, rows, S, T, HALO, hkey, ztdst, zkey):
                n = (ntile - 1) * 128 + rows
                for i in range(ntile):
                    rr = rows if i == ntile - 1 else 128
                    dma("sp", X4[0:rr, 0, :], src[i * 128:i * 128 + rr, :], [skey], ["X4"])
                    rmsnorm(X4[0:rr, 0, :], H4[0:rr, 0, :], rr)
                    transpose_to(H4[0:rr, 0, :], rr, HT[:, :, i * 128:i * 128 + rr], "HT", "H4")
                for f in range(24):
                    fp = f % 2
                    fs = slice(f * 128, (f + 1) * 128)
                    pa, pak = [(psA, "psA"), (S1, "S1")][fp]
                    pg_, pgk = [(psB, "psB"), (S2, "S2")][fp]
                    AB_, abk = [(ABUF, "ABUF"), (OC, "OC")][fp]
                    T1_, t1k = [(T1, "T1"), (OA, "OA")][fp]
                    T2_, t2k = [(T2, "T2"), (OB, "OB")][fp]
                    ZS_, zsk = [(ZST, "ZST"), (AOUT, "AOUT")][fp]
                    AB3 = AB_[:, 0:S * (T + 2)].rearrange("p (s t) -> p s t", s=S)
                    for k in range(8):
                        mm(pa[:, 0:n], WAa[:, k, fs], HT[:, k, 0:n], k == 0, k == 7, ["WA", "HT"], [pak])
                    for k in range(8):
                        mm(pg_[:, 0:n], WBb[:, k, fs], HT[:, k, 0:n], k == 0, k == 7, ["WB", "HT"], [pgk])
                    cp("act", AB3[:, :, 2:T + 2], pa[:, 0:n].rearrange("p (s t) -> p s t", s=S), [pak], [abk])
                    cp("pool", AB3[:, :, 0:2], HALO[:, f, :, :], [hkey], [abk])
                    t13 = T1_[:, 0:n].rearrange("p (s t) -> p s t", s=S)
                    ts("dve", t13, AB3[:, :, 0:T], fcwT[:, 0, f:f + 1], fcbT[:, f:f + 1], ALU.mult, ALU.add, [abk, "fcwT", "fcbT"], [t1k])
                    stt("dve", t13, AB3[:, :, 1:T + 1], fcwT[:, 1, f:f + 1], t13, ALU.mult, ALU.add, [abk, "fcwT", t1k], [t1k])
                    stt("dve", t13, AB3[:, :, 2:T + 2], fcwT[:, 2, f:f + 1], t13, ALU.mult, ALU.add, [abk, "fcwT", t1k], [t1k])
                    cp("pool", HALO[:, f, :, :], AB3[:, :, T:T + 2], [abk], [hkey])
                    act(T2_[:, 0:n], T1_[:, 0:n], AF.Gelu, [t1k], [t2k])
                    tt("dve", ZS_[:, 0:n], T2_[:, 0:n], pg_[:, 0:n], ALU.mult, [t2k, pgk], [zsk])
                    dma("sp", ztdst[fs, :], ZS_[:, 0:n], [zsk], [zkey], key="st_ZST%d" % fp)

            for g in range(8):
                up_group(srcp[g * 512:(g + 1) * 512, :], skeyp, 4, 128, 1, 512, HALOp, "HALOp", ZTp[:, g * 512:(g + 1) * 512], "ZTp")
            up_group(srcs, skeys, 1, 32, 4, 8, HALOs, "HALOs", ZTs, "ZTs")
            for j in range(2):
                dma_nc("sp", o_pfc[l, 0, j].rearrange("(f p) -> p f", p=128), HALOp[:, :, 0, j], ["HALOp"], (), key="st_halo")
                for b in range(4):
                    dma_nc("sp", o_sfc[l, b, j].rearrange("(f p) -> p f", p=128), HALOs[:, :, b, j], ["HALOs"], (), key="st_halo")

            WD = WA[:, 0:24 * 1024].rearrange("p (f n) -> p f n", n=1024)
            for f in range(24):
                wload(WD[:, f, :], w_dn[l, f * 128:(f + 1) * 128, :], "WA")
            if final:
                load_gamma(fin_g[0:1, :])

            ZTt2 = HT[:].rearrange("p a b -> p (a b)")[:, 0:3072].rearrange("p (f r) -> p f r", f=24)

            def down_tile(ztsrc, zkey, rows, xsrc, xkey, xdst, xdkey, par=0):
                zt, ztk = [(ATt, "ATt"), (ZTt2, "HT")][par]
                dma("sp", zt[:, :, 0:rows], ztsrc.rearrange("(f p) r -> p f r", p=128), [zkey], [ztk])
                dma("sp", X4[0:rows, 0, :], xsrc, [xkey], ["X4"])
                for pn in range(2):
                    pp, ppk = [(psA, "psA"), (psB, "psB")][pn]
                    for f in range(24):
                        mm(pp[0:rows, :], zt[:, f, 0:rows], WD[:, f, pn * 512:(pn + 1) * 512], f == 0, f == 23, [ztk, "WA"], [ppk])
                    tt("dve", X4[0:rows, 1, pn * 512:(pn + 1) * 512], pp[0:rows, :], X4[0:rows, 0, pn * 512:(pn + 1) * 512], ALU.add,
                       [ppk, "X4"], ["X4"])
                if final:
                    xt = X4[0:rows, 1, :]
                    tt("dve", SQ[0:rows, :], xt, xt, ALU.mult, ["X4"], ["SQ"])
                    rsum(sm[0:rows, 0:1], SQ[0:rows, :], ["SQ"], ["sm"])
                    ts("dve", sm[0:rows, 0:1], sm[0:rows, 0:1], 1.0 / D, EPS, ALU.mult, ALU.add, ["sm"], ["sm"])
                    act(sm[0:rows, 1:2], sm[0:rows, 0:1], AF.Ln, ["sm"], ["sm"])
                    act(sm[0:rows, 2:3], sm[0:rows, 1:2], AF.Exp, ["sm"], ["sm"], scale=-0.5)
                    stt("dve", X4[0:rows, 2, :], xt, sm[0:rows, 2:3], GB[0:rows, :], ALU.mult, ALU.mult, ["X4", "sm", "GB"], ["X4"])
                    dma("sp", xdst, X4[0:rows, 2, :], ["X4"], [xdkey], key="st_X4")
                else:
                    dma("sp", xdst, X4[0:rows, 1, :], ["X4"], [xdkey], key="st_X4")

            for t in range(NTP):
                r0 = t * 128
                down_tile(ZTp[:, r0:r0 + 128], "ZTp", 128, srcp[r0:r0 + 128, :], skeyp, dstp[r0:r0 + 128, :], dkeyp, t % 2)
            down_tile(ZTs[:, :], "ZTs", 32, srcs[:, :], skeys, dsts[:, :], dkeys)

        ffn(0, XAp, XAs, "XAp", "XAs", XBp, XBs, "XBp", "XBs", False)

        WG = WA[:, 0:8 * DIN].rearrange("p (k n) -> p k n", n=DIN)
        for k in range(8):
            wload(WG[:, k, 0:DRNN], w_gate[k * 128:(k + 1) * 128, :], "WA")
            wload(WG[:, k, DRNN:2 * DRNN], w_x[k * 128:(k + 1) * 128, :], "WA")
        BDA = WB[:, 0:3840].rearrange("p (m j c) -> p m j c", m=10, j=3)
        BDI = WB[:, 3840:7680].rearrange("p (m j c) -> p m j c", m=10, j=3)
        WLO = WB[:, 7680:7680 + 10240].rearrange("p (k n) -> p k n", n=1024)
        memset("pool", WB[:, 0:7680], 0.0, ["WB"])
        for nb in range(16):
            c_lo = nb * 80
            segs = []
            r = c_lo
            while r < c_lo + 80:
                kc = r // 128
                e = min(c_lo + 80, (kc + 1) * 128)
                segs.append((r, e, kc))
                r = e
            for (ra, rb, kc) in segs:
                for (oa, ob, m) in segs:
                    j = kc - (m - 1)
                    wload(BDA[ra - kc * 128:rb - kc * 128, m, j, oa - m * 128:ob - m * 128],
                          lwa[nb, ra - c_lo:rb - c_lo, oa - c_lo:ob - c_lo], "WB")
                    wload(BDI[ra - kc * 128:rb - kc * 128, m, j, oa - m * 128:ob - m * 128],
                          lwi[nb, ra - c_lo:rb - c_lo, oa - c_lo:ob - c_lo], "WB")
        for k in range(10):
            wload(WLO[:, k, :], w_lo[k * 128:(k + 1) * 128, :], "WB")
        load_gamma(mix_g[1:2, :])
        for j in range(4):
            dma_nc("sp", lcwT[:, j, :], lcw[j].rearrange("(m p) -> p m", p=128), (), ["lcwT"], key="ld_lcw")
        dma_nc("sp", lcbT[:, :], lcb.rearrange("o (m p) -> p (o m)", p=128), (), ["lcbT"])
        dma_nc("sp", lbaT[:, :], lba.rearrange("o (m p) -> p (o m)", p=128), (), ["lbaT"])
        dma_nc("sp", lbiT[:, :], lbi.rearrange("o (m p) -> p (o m)", p=128), (), ["lbiT"])
        dma_nc("sp", lsc[:, :], llam.rearrange("o (m p) -> p (o m)", p=128), (), ["lsc"])
        ts("dve", nlbaT[:, :], lbaT[:, :], -1.0, None, ALU.mult, ALU.bypass, ["lbaT"], ["lbaT"])
        ts("dve", nlbiT[:, :], lbiT[:, :], -1.0, None, ALU.mult, ALU.bypass, ["lbiT"], ["lbiT"])
        act(lsc[:, :], lsc[:, :], AF.Exp, ["lsc"], ["lsc"], scale=-1.0)
        act(lsc[:, :], lsc[:, :], AF.Ln, ["lsc"], ["lsc"], bias=1.0)
        ts("dve", lsc[:, :], lsc[:, :], -8.0, None, ALU.mult, ALU.bypass, ["lsc"], ["lsc"])
        memset("pool", UHp[:], 0.0, ["UHp"]); memset("pool", HSTp[:], 0.0, ["HSTp"])
        for b in range(4):
            for j in range(3):
                dma_nc("sp", UHs[:, :, b, j], s_lc[b, j].rearrange("(m p) -> p m", p=128), (), ["UHs"], key="ld_uhs")
            dma_nc("sp", HSTs[:, :, b], s_lh[b].rearrange("(m p) -> p m", p=128), (), ["HSTs"], key="ld_hsts")

        def lru_group(src, skey, ntile, rows, S, T, UH, ukey, HST, hkey, dst, dkey):
            n = (ntile - 1) * 128 + rows
            for i in range(ntile):
                rr = rows if i == ntile - 1 else 128
                dma("sp", X4[0:rr, i, :], src[i * 128:i * 128 + rr, :], [skey], ["X4"])
                rmsnorm(X4[0:rr, i, :], H4[0:rr, 0, :], rr)
                transpose_to(H4[0:rr, 0, :], rr, HT[:, :, i * 128:i * 128 + rr], "HT", "H4")
            UB3 = UBUF[:, 0:S * (T + 3)].rearrange("p (s t) -> p s t", s=S)
            for m in range(10):
                ms = slice(m * 128, (m + 1) * 128)
                for k in range(8):
                    mm(psA[:, 0:n], WG[:, k, ms], HT[:, k, 0:n], k == 0, k == 7, ["WA", "HT"], ["psA"])
                for k in range(8):
                    mm(psB[:, 0:n], WG[:, k, DRNN + m * 128:DRNN + (m + 1) * 128], HT[:, k, 0:n], k == 0, k == 7, ["WA", "HT"], ["psB"])
                act(GT[:, m, 0:n], psA[:, 0:n], AF.Gelu, ["psA"], ["GT"])
                cp("act", UB3[:, :, 3:T + 3], psB[:, 0:n].rearrange("p (s t) -> p s t", s=S), ["psB"], ["UBUF"])
                cp("pool", UB3[:, :, 0:3], UH[:, m, :, :], [ukey], ["UBUF"])
                x3 = XC[:, m, 0:n].rearrange("p (s t) -> p s t", s=S)
                ts("dve", x3, UB3[:, :, 0:T], lcwT[:, 0, m:m + 1], lcbT[:, m:m + 1], ALU.mult, ALU.add, ["UBUF", "lcwT", "lcbT"], ["F0"])
                for j in range(1, 4):
                    stt("dve", x3, UB3[:, :, j:T + j], lcwT[:, j, m:m + 1], x3, ALU.mult, ALU.add, ["UBUF", "lcwT", "F0"], ["F0"])
                cp("pool", UH[:, m, :, :], UB3[:, :, T:T + 3], ["UBUF"], [ukey])
                cp("pool", XCb[:, m, 0:n], XC[:, m, 0:n], ["F0"], ["XCb"])
            for m in range(10):
                ms = slice(m * 128, (m + 1) * 128)
                js = [j for j in range(3) if 0 <= m - 1 + j < 10]
                for i, j in enumerate(js):
                    mm(psA[:, 0:n], BDA[:, m, j, :], XCb[:, m - 1 + j, 0:n], i == 0, i == len(js) - 1, ["WB", "XCb"], ["psA"])
                for i, j in enumerate(js):
                    mm(psB[:, 0:n], BDI[:, m, j, :], XCb[:, m - 1 + j, 0:n], i == 0, i == len(js) - 1, ["WB", "XCb"], ["psB"])
                act(T1[:, 0:n], psA[:, 0:n], AF.Exp, ["psA", "lbaT"], ["T1"], bias=nlbaT[:, m:m + 1], scale=-1.0)
                act(T2[:, 0:n], psB[:, 0:n], AF.Exp, ["psB", "lbiT"], ["T2"], bias=nlbiT[:, m:m + 1], scale=-1.0)
                ts("dve", T1[:, 0:n], T1[:, 0:n], 1.0, None, ALU.add, ALU.bypass, ["T1"], ["T1"])
                ts("dve", T2[:, 0:n], T2[:, 0:n], 1.0, None, ALU.add, ALU.bypass, ["T2"], ["T2"])
                recip(T1[:, 0:n], T1[:, 0:n], ["T1"], ["T1"])
                recip(T2[:, 0:n], T2[:, 0:n], ["T2"], ["T2"])
                act(T1[:, 0:n], T1[:, 0:n], AF.Exp, ["T1", "lsc"], ["T1"], scale=lsc[:, m:m + 1])
                tt("dve", OA[:, 0:n], T1[:, 0:n], T1[:, 0:n], ALU.mult, ["T1"], ["OA"])
                ts("dve", OA[:, 0:n], OA[:, 0:n], -1.0, 1.0, ALU.mult, ALU.add, ["OA"], ["OA"])
                ts("dve", OA[:, 0:n], OA[:, 0:n], 1e-30, None, ALU.max, ALU.bypass, ["OA"], ["OA"])
                act(OA[:, 0:n], OA[:, 0:n], AF.Ln, ["OA"], ["OA"])
                act(OA[:, 0:n], OA[:, 0:n], AF.Exp, ["OA"], ["OA"], scale=0.5)
                tt("dve", T2[:, 0:n], T2[:, 0:n], XC[:, m, 0:n], ALU.mult, ["T2", "F0"], ["T2"])
                tt("dve", T2[:, 0:n], T2[:, 0:n], OA[:, 0:n], ALU.mult, ["T2", "OA"], ["T2"])
                for s in range(S):
                    cs = slice(s * T, (s + 1) * T)
                    P.op("dve", lambda e, o=HS[:, cs], a=T1[:, cs], b=T2[:, cs], ini=HST[:, m, s:s + 1]:
                         e.tensor_tensor_scan(out=o, data0=a, data1=b, initial=ini, op0=ALU.mult, op1=ALU.add),
                         ["T1", "T2", hkey], ["HS"])
                    cp("dve", HST[:, m, s:s + 1], HS[:, s * T + T - 1:s * T + T], ["HS"], [hkey])
                tt("dve", YT[:, m, 0:n], HS[:, 0:n], GT[:, m, 0:n], ALU.mult, ["HS", "GT"], ["YT"])
            for i in range(ntile):
                rr = rows if i == ntile - 1 else 128
                for pn in range(2):
                    for m in range(10):
                        mm(psA[0:rr, :], YT[:, m, i * 128:i * 128 + rr], WLO[:, m, pn * 512:(pn + 1) * 512], m == 0, m == 9, ["YT", "WB"], ["psA"])
                    tt("dve", X4[0:rr, i, pn * 512:(pn + 1) * 512], psA[0:rr, :], X4[0:rr, i, pn * 512:(pn + 1) * 512], ALU.add,
                       ["psA", "X4"], ["X4"])
                dma("sp", dst[i * 128:i * 128 + rr, :], X4[0:rr, i, :], ["X4"], [dkey], key="st_X4")

        for g in range(16):
            lru_group(XBp[g * 256:(g + 1) * 256, :], "XBp", 2, 128, 1, 256, UHp, "UHp", HSTp, "HSTp", XAp[g * 256:(g + 1) * 256, :], "XAp")
        lru_group(XBs, "XBs", 1, 32, 4, 8, UHs, "UHs", HSTs, "HSTs", XAs, "XAs")
        for j in range(3):
            dma_nc("sp", o_plc[0, j].rearrange("(m p) -> p m", p=128), UHp[:, :, 0, j], ["UHp"], (), key="st_lru")
            for b in range(4):
                dma_nc("sp", o_slc[b, j].rearrange("(m p) -> p m", p=128), UHs[:, :, b, j], ["UHs"], (), key="st_lru")
        dma_nc("sp", o_plh[0].rearrange("(m p) -> p m", p=128), HSTp[:, :, 0], ["HSTp"], (), key="st_lru")
        for b in range(4):
            dma_nc("sp", o_slh[b].rearrange("(m p) -> p m", p=128), HSTs[:, :, b], ["HSTs"], (), key="st_lru")

        ffn(1, XAp, XAs, "XAp", "XAs", y_p, y_s, "y_p", "y_s", True)

        P.run()
    return nc


_CACHE = {}


def _consts():
    half = 8
    inv = (np.float32(500000.0) ** (-np.arange(half, dtype=np.float32) / np.float32(half))).astype(np.float32)
    posp = np.arange(L, dtype=np.float32)
    angp = posp[:, None] * inv[None, :]
    poss = (8192 + np.arange(8)).astype(np.float32)
    angs = np.tile(poss[:, None] * inv[None, :], (4, 1)).astype(np.float32)
    tri = (np.arange(128)[:, None] <= np.arange(128)[None, :]).astype(np.float32)
    return {
        "k_ident": np.eye(128, dtype=np.float32).astype(ml_dtypes.bfloat16),
        "k_tri": tri, "k_ones": np.ones((128, 128), np.float32),
        "k_cosp": np.cos(angp).astype(np.float32), "k_sinp": np.sin(angp).astype(np.float32),
        "k_coss": np.cos(angs).astype(np.float32), "k_sins": np.sin(angs).astype(np.float32),
        "k_piota": np.arange(128, dtype=np.float32).reshape(128, 1),
    }


def kernel(**inp):
    f = lambda a: np.ascontiguousarray(np.asarray(a, dtype=np.float32))
    if "nc" not in _CACHE:
        _CACHE["nc"] = build()
    nc = _CACHE["nc"]
    consts = _consts()
    shared = {
        "c_all": np.concatenate([
            f(inp["cache_fox_k"]).reshape(NPOOL * 128, 512), f(inp["cache_fox_v"]).reshape(NPOOL * 128, 512),
            f(inp["cache_fox_logf"]).reshape(NPOOL * 128, 8),
            f(inp["cache_diff_k"]).reshape(NPOOL * 128, 512), f(inp["cache_diff_v"]).reshape(NPOOL * 128, 512)], axis=1),
        "mix_g": f(inp["mix_norm_g"]), "ffn_g": f(inp["ffn_norm_g"]), "fin_g": f(inp["final_norm_g"]).reshape(1, D),
        "w_in": f(inp["ab_w_in"])[0], "b_f": f(inp["ab_b_f"]).reshape(1, 8),
        "lq1": f(inp["ab_lam_q1"]).reshape(1, 64), "lk1": f(inp["ab_lam_k1"]).reshape(1, 64),
        "lq2": f(inp["ab_lam_q2"]).reshape(1, 64), "lk2": f(inp["ab_lam_k2"]).reshape(1, 64),
        "subg": f(inp["ab_subln_g"]).reshape(128, 1), "w_out": f(inp["ab_w_out"])[0],
        "w_gate": f(inp["lru_w_gate"])[0], "w_x": f(inp["lru_w_x"])[0],
        "lcw": f(inp["lru_conv_w"])[0], "lcb": f(inp["lru_conv_b"]).reshape(1, DRNN),
        "lwa": f(inp["lru_w_a"])[0], "lba": f(inp["lru_b_a"]).reshape(1, DRNN),
        "lwi": f(inp["lru_w_i"])[0], "lbi": f(inp["lru_b_i"]).reshape(1, DRNN),
        "llam": f(inp["lru_lambda"]).reshape(1, DRNN), "w_lo": f(inp["lru_w_out"])[0],
        "w_a": f(inp["ffn_w_a"]), "w_b": f(inp["ffn_w_b"]), "fcw": f(inp["ffn_conv_w"]), "fcb": f(inp["ffn_conv_b"]),
        "w_dn": f(inp["ffn_w_down"]),
    }
    shared.update(consts)
    xp_all = f(inp["x_prompt"]); xs_all = f(inp["x_sample"])
    ptab = np.ascontiguousarray(np.asarray(inp["page_table"], dtype=np.int32))
    slc = f(inp["state_lru_conv"])[0]; slh = f(inp["state_lru_h"])[0]; sfc = f(inp["state_ffn_conv"])
    in_maps = []
    for c in range(8):
        m = dict(shared)
        m["xp"] = xp_all[c % 4]
        m["xs"] = xs_all[4 * c:4 * c + 4].reshape(32, D)
        m["pt"] = ptab[4 * c:4 * c + 4].reshape(1, 256)
        m["s_lc"] = np.ascontiguousarray(slc[4 * c:4 * c + 4]); m["s_lh"] = np.ascontiguousarray(slh[4 * c:4 * c + 4])
        m["s_fc"] = np.ascontiguousarray(sfc[:, 4 * c:4 * c + 4])
        in_maps.append(m)
    res = run_bass_kernel_spmd(nc, in_maps, core_ids=list(range(8))).results
    R = lambda name, cs: [res[c][name] for c in cs]
    pc = range(4); ac = range(8)
    y_prompt = np.stack(R("y_p", pc)).reshape(4, L, D)
    y_sample = np.concatenate(R("y_s", ac)).reshape(32, 8, D)
    p_fk = np.stack(R("o_pfk", pc)).reshape(1, 4, L, 8, 64); p_fv = np.stack(R("o_pfv", pc)).reshape(1, 4, L, 8, 64)
    p_fl = np.stack(R("o_pfl", pc)).reshape(1, 4, L, 8)
    p_dk = np.stack(R("o_pdk", pc)).reshape(1, 4, L, 4, 128); p_dv = np.stack(R("o_pdv", pc)).reshape(1, 4, L, 4, 128)
    p_lc = np.concatenate(R("o_plc", pc)).reshape(1, 4, 3, DRNN); p_lh = np.concatenate(R("o_plh", pc)).reshape(1, 4, DRNN)
    p_fc = np.concatenate(R("o_pfc", pc), axis=1).reshape(2, 4, 2, DFF)
    s_fk = np.concatenate(R("o_sfk", ac)).reshape(1, 32, 8, 8, 64); s_fv = np.concatenate(R("o_sfv", ac)).reshape(1, 32, 8, 8, 64)
    s_fl = np.concatenate(R("o_sfl", ac)).reshape(1, 32, 8, 8)
    s_dk = np.concatenate(R("o_sdk", ac)).reshape(1, 32, 8, 4, 128); s_dv = np.concatenate(R("o_sdv", ac)).reshape(1, 32, 8, 4, 128)
    s_lc = np.concatenate(R("o_slc", ac)).reshape(1, 32, 3, DRNN); s_lh = np.concatenate(R("o_slh", ac)).reshape(1, 32, DRNN)
    s_fc = np.concatenate(R("o_sfc", ac), axis=1).reshape(2, 32, 2, DFF)
    outs = (y_prompt, y_sample, p_fk, p_fv, p_fl, p_dk, p_dv, p_lc, p_lh, p_fc,
            s_fk, s_fv, s_fl, s_dk, s_dv, s_lc, s_lh, s_fc)
    return tuple(np.ascontiguousarray(o, dtype=np.float32) for o in outs)
```
